# Optimizing a Trainium2 kernel written in Bass

```python
import jax, jax.numpy as jnp
from jax import lax
import numpy as np

D_MODEL = 2048
BATCH = 1
SEQ = 8192
DEPTH = 1

D_MIX = D_MODEL
D_SSD = D_MIX // 2
SSD_HEAD_DIM = 64
SSD_HEADS = D_SSD // SSD_HEAD_DIM
SSD_GROUPS = 2
SSD_HEADS_PER_GROUP = SSD_HEADS // SSD_GROUPS
SSD_STATE = 128
SSD_CONV = 5
SSD_CHUNK = 128
SSD_CONV_DIM = D_SSD + 2 * SSD_GROUPS * SSD_STATE
D_POOL = D_MIX - D_SSD
POOL_WINDOWS = (2, 4, 8, 16)
POOL_GROUPS = len(POOL_WINDOWS)
POOL_GROUP_DIM = D_POOL // POOL_GROUPS
D_IN_PROJ = D_SSD + SSD_CONV_DIM + 2 * SSD_HEADS + D_POOL
D_FF = 5632
FFN_CONV = 3
D_PLE = 256
EPS = 1e-6

kernel_name = "hybrid_ssd_pool_convglu_encoder_block"

F32 = jnp.float32


def rmsnorm(x, g):
    xf = x.astype(F32)
    y = xf * lax.rsqrt(jnp.mean(xf * xf, axis=-1, keepdims=True) + EPS)
    return (y * g.astype(F32)).astype(x.dtype)


def dwconv_centred(u, w, bias):
    k = w.shape[0]
    L = u.shape[1]
    half = k // 2
    up = jnp.pad(u, ((0, 0), (half, half), (0, 0)))
    wf = w.astype(u.dtype)
    y = up[:, 0:L] * wf[0]
    for j in range(1, k):
        y = y + up[:, j:j + L] * wf[j]
    return y + bias.astype(u.dtype)


def ssd_scan(xs, dt, a, bm, cm):
    b, L, g, e, pdim = xs.shape
    n = bm.shape[-1]
    c = L // SSD_CHUNK
    q = SSD_CHUNK
    xdt = (xs.astype(F32) * dt[..., None]).reshape(b, c, q, g, e, pdim)
    da = jnp.moveaxis((dt * a).reshape(b, c, q, g, e), 2, -1)
    acum = jnp.cumsum(da, axis=-1)
    bmc = bm.astype(F32).reshape(b, c, q, g, n)
    cmc = cm.astype(F32).reshape(b, c, q, g, n)
    causal = jnp.tril(jnp.ones((q, q), dtype=bool))
    seg = acum[..., :, None] - acum[..., None, :]
    decay = jnp.exp(jnp.where(causal, seg, -jnp.inf))
    cb = jnp.einsum("bclgn,bcsgn->bcgls", cmc, bmc)
    y_diag = jnp.einsum("bcgls,bcgels,bcsgep->bclgep", cb, decay, xdt)
    decay_to_end = jnp.exp(acum[..., -1:] - acum)
    states = jnp.einsum("bclgn,bcgel,bclgep->bcgepn", bmc, decay_to_end, xdt)
    chunk_decay = jnp.exp(acum[..., -1])

    def step(h, inp):
        s, d = inp
        return h * d[..., None, None] + s, h

    h0 = jnp.zeros((b, g, e, pdim, n), F32)
    _, h_in = lax.scan(step, h0, (jnp.moveaxis(states, 1, 0), jnp.moveaxis(chunk_decay, 1, 0)))
    h_in = jnp.moveaxis(h_in, 0, 1)
    y_off = jnp.einsum("bclgn,bcgepn,bcgel->bclgep", cmc, h_in, jnp.exp(acum))
    return (y_diag + y_off).reshape(b, L, g, e, pdim)


def ssd_mixer(z, xbc, dt_raw, conv_w, conv_b, dt_bias, a_log, d_skip, norm_w):
    b, L, _ = z.shape
    G, E = SSD_GROUPS, SSD_HEADS_PER_GROUP
    xbc = jax.nn.silu(dwconv_centred(xbc, conv_w, conv_b))
    xs, bm, cm = jnp.split(xbc, [D_SSD, D_SSD + G * SSD_STATE], axis=-1)
    xs = xs.reshape(b, L, G, E, SSD_HEAD_DIM)
    bm = bm.reshape(b, L, G, SSD_STATE)
    cm = cm.reshape(b, L, G, SSD_STATE)
    dt = jax.nn.softplus(dt_raw.astype(F32).reshape(b, L, 2, SSD_HEADS) + dt_bias.astype(F32))
    dt = dt.reshape(b, L, 2, G, E)
    a = (-jnp.exp(a_log.astype(F32))).reshape(2, G, E)
    flip = lambda t: jnp.flip(t, axis=1)
    y_fwd = ssd_scan(xs, dt[:, :, 0], a[0], bm, cm)
    y_bwd = flip(ssd_scan(flip(xs), flip(dt[:, :, 1]), a[1], flip(bm), flip(cm)))
    y = y_fwd + y_bwd + xs.astype(F32) * d_skip.astype(F32).reshape(G, E)[:, :, None]
    y = y.reshape(b, L, D_SSD) * jax.nn.silu(z.astype(F32))
    y = y.reshape(b, L, G, D_SSD // G)
    y = y * lax.rsqrt(jnp.mean(y * y, axis=-1, keepdims=True) + EPS)
    return (y.reshape(b, L, D_SSD) * norm_w.astype(F32)).astype(z.dtype)


def pool_mixer(u, w, scale):
    b, L, _ = u.shape
    uf = u.astype(F32).reshape(b, L, POOL_GROUPS, POOL_GROUP_DIM)
    cs = jnp.concatenate([jnp.zeros((b, 1, POOL_GROUPS, POOL_GROUP_DIM), F32),
                          jnp.cumsum(uf, axis=1)], axis=1)
    t = jnp.arange(L)
    means = []
    for gi, k in enumerate(POOL_WINDOWS):
        lo = jnp.clip(t - k // 2, 0, L)
        hi = jnp.clip(t + (k - k // 2), 0, L)
        csg = cs[:, :, gi]
        cnt = (hi - lo).astype(F32)[None, :, None]
        means.append((csg[:, hi] - csg[:, lo]) / cnt)
    mixed = jnp.stack(means, axis=2) - uf
    y = jnp.einsum("blgc,gcd->blgd", mixed, w.astype(F32))
    return (y.reshape(b, L, D_POOL) * scale.astype(F32)).astype(u.dtype)


def conv_glu(hn, w_up, conv_w, conv_b, w_down):
    up = hn @ w_up
    gate, val = jnp.split(up, 2, axis=-1)
    gate = dwconv_centred(gate, conv_w, conv_b)
    return (jax.nn.gelu(gate, approximate=True) * val) @ w_down


def setup_inputs(seed: int = 0) -> dict:
    key = jax.random.key(seed)
    ks = jax.random.split(key, 24)
    nrm = lambda k, shape, s: jax.random.normal(k, shape, F32) * s
    gain = lambda k: 1.0 + 0.02 * jax.random.normal(k, (DEPTH, D_MODEL), F32)
    dt0 = jnp.exp(jax.random.uniform(ks[6], (DEPTH, 2, SSD_HEADS), F32,
                                     np.log(1e-3).astype(np.float32), np.log(1e-1).astype(np.float32)))
    dt_bias = dt0 + jnp.log(-jnp.expm1(-dt0))
    return {
        "x": jax.random.normal(ks[0], (BATCH, SEQ, D_MODEL), F32),
        "p": jax.random.normal(ks[1], (DEPTH, BATCH, SEQ, D_PLE), F32),
        "mix_norm_pre": gain(ks[2]),
        "mix_norm_post": gain(ks[3]),
        "w_in": nrm(ks[4], (DEPTH, D_MODEL, D_IN_PROJ), D_MODEL ** -0.5),
        "ssd_conv_w": nrm(ks[5], (DEPTH, SSD_CONV, SSD_CONV_DIM), SSD_CONV ** -0.5),
        "ssd_conv_b": nrm(ks[7], (DEPTH, SSD_CONV_DIM), 0.02),
        "ssd_dt_bias": dt_bias,
        "ssd_a_log": jnp.log(jax.random.uniform(ks[8], (DEPTH, 2, SSD_HEADS), F32, 1.0, 16.0)),
        "ssd_d": 1.0 + 0.1 * jax.random.normal(ks[9], (DEPTH, SSD_HEADS), F32),
        "ssd_norm": 1.0 + 0.02 * jax.random.normal(ks[10], (DEPTH, D_SSD), F32),
        "pool_w": nrm(ks[11], (DEPTH, POOL_GROUPS, POOL_GROUP_DIM, POOL_GROUP_DIM), POOL_GROUP_DIM ** -0.5),
        "pool_scale": 1.0 + 0.1 * jax.random.normal(ks[12], (DEPTH, D_POOL), F32),
        "w_out": nrm(ks[13], (DEPTH, D_MIX, D_MODEL), D_MIX ** -0.5),
        "ffn_norm_pre": gain(ks[14]),
        "ffn_norm_post": gain(ks[15]),
        "w_ffn_up": nrm(ks[16], (DEPTH, D_MODEL, 2 * D_FF), D_MODEL ** -0.5),
        "ffn_conv_w": nrm(ks[17], (DEPTH, FFN_CONV, D_FF), FFN_CONV ** -0.5),
        "ffn_conv_b": nrm(ks[18], (DEPTH, D_FF), 0.02),
        "w_ffn_down": nrm(ks[19], (DEPTH, D_FF, D_MODEL), D_FF ** -0.5),
        "ple_norm_pre": gain(ks[20]),
        "w_ple_gate": nrm(ks[21], (DEPTH, D_MODEL, D_MODEL), D_MODEL ** -0.5),
        "w_ple": nrm(ks[22], (DEPTH, D_PLE, D_MODEL), D_PLE ** -0.5),
        "ple_norm_post": gain(ks[23]),
    }


def reference(x, p, mix_norm_pre, mix_norm_post, w_in, ssd_conv_w, ssd_conv_b, ssd_dt_bias,
              ssd_a_log, ssd_d, ssd_norm, pool_w, pool_scale, w_out, ffn_norm_pre, ffn_norm_post,
              w_ffn_up, ffn_conv_w, ffn_conv_b, w_ffn_down, ple_norm_pre, w_ple_gate, w_ple,
              ple_norm_post):
    h = x
    split_at = [D_SSD, D_SSD + SSD_CONV_DIM, D_SSD + SSD_CONV_DIM + 2 * SSD_HEADS]
    for i in range(DEPTH):
        hn = rmsnorm(h, mix_norm_pre[i])
        proj = hn @ w_in[i]
        z, xbc, dt_raw, u = jnp.split(proj, split_at, axis=-1)
        y_ssd = ssd_mixer(z, xbc, dt_raw, ssd_conv_w[i], ssd_conv_b[i], ssd_dt_bias[i],
                          ssd_a_log[i], ssd_d[i], ssd_norm[i])
        y_pool = pool_mixer(u, pool_w[i], pool_scale[i])
        mix = jnp.concatenate([y_ssd, y_pool], axis=-1) @ w_out[i]
        h = h + rmsnorm(mix, mix_norm_post[i])
        hn = rmsnorm(h, ffn_norm_pre[i])
        ff = conv_glu(hn, w_ffn_up[i], ffn_conv_w[i], ffn_conv_b[i], w_ffn_down[i])
        h = h + rmsnorm(ff, ffn_norm_post[i])
        gate = jax.nn.sigmoid(rmsnorm(h, ple_norm_pre[i]) @ w_ple_gate[i])
        h = h + rmsnorm(gate * (p[i] @ w_ple[i]), ple_norm_post[i])
    return h
```

```python
import numpy as np
import concourse.bass as bass
import concourse.mybir as mybir
from concourse.bass_utils import run_bass_kernel_spmd
from contextlib import ExitStack

F32 = mybir.dt.float32
BF16 = mybir.dt.bfloat16
AF = mybir.ActivationFunctionType
ALU = mybir.AluOpType
AX = mybir.AxisListType

NCORE = 8
L = 8192
TOK = 1024
D = 2048
NB = 16
DFF = 5632
NFB = 44
EPS = 1e-6
NEG = -30000.0
HL = 8


class Buf:
    __slots__ = ("name", "w", "r")

    def __init__(self, name):
        self.name = name
        self.w = None
        self.r = {}


class Sched:
    CE = ["pe", "act", "dve", "pool"]
    ENG = ["pe", "act", "dve", "pool", "sp"]

    def __init__(self, nc, es, n_dsem=48, same_eng_sync=True):
        self.nc = nc
        self.es = es
        self.ops = {e: [] for e in self.ENG}
        self.sems = {}
        for e in self.CE:
            self.sems[e] = es.enter_context(nc.semaphore("s_" + e))
        self.n_dsem = n_dsem
        for i in range(n_dsem):
            self.sems[("d", i)] = es.enter_context(nc.semaphore("d%d" % i))
        self.cnt = {e: 0 for e in self.CE}
        self.dcnt = [0] * n_dsem
        self.dnext = 0
        self.dnext_sw = 0
        self.acnt = {}
        self.waited = {e: {} for e in self.ENG}
        self.same = same_eng_sync
        self.nbuf = 0

    def buf(self, name=None):
        self.nbuf += 1
        return Buf(name or "b%d" % self.nbuf)

    def bufs(self, n, name="b"):
        return [self.buf("%s%d" % (name, i)) for i in range(n)]

    def inherit(self, new, olds):
        for o in olds:
            toks = list(o.r.items())
            if o.w is not None:
                toks.append(o.w)
            for k, v in toks:
                if new.r.get(k, 0) < v:
                    new.r[k] = v

    def _deps(self, reads, writes):
        deps = {}

        def add(k, v):
            if deps.get(k, 0) < v:
                deps[k] = v

        for b in reads:
            if b.w is not None:
                add(*b.w)
        for b in writes:
            if b.w is not None:
                add(*b.w)
            for k, v in b.r.items():
                add(k, v)
        return deps

    def _waits(self, eng, deps):
        waits = []
        wd = self.waited[eng]
        for k, v in deps.items():
            if k == eng and (eng == "pe" or not self.same):
                continue
            if wd.get(k, 0) >= v:
                continue
            wd[k] = v
            waits.append((k, v))
        return waits

    def _mark(self, tok, reads, writes):
        k, v = tok
        for b in reads:
            if b.r.get(k, 0) < v:
                b.r[k] = v
        for b in writes:
            b.w = tok
            b.r = {}

    def op(self, eng, fn, reads=(), writes=()):
        deps = self._deps(reads, writes)
        waits = self._waits(eng, deps)
        self.cnt[eng] += 1
        tok = (eng, self.cnt[eng])
        self.ops[eng].append((waits, fn, tok[0], 1))
        self._mark(tok, reads, writes)
        return tok

    def dma(self, q, out, in_, reads=(), writes=(), **kw):
        deps = self._deps(reads, writes)
        nsw = self.n_dsem // 3
        if q == "pool":
            i = self.dnext_sw
            self.dnext_sw = (i + 1) % nsw
        else:
            i = nsw + self.dnext
            self.dnext = (self.dnext + 1) % (self.n_dsem - nsw)
        if self.dcnt[i] > 0:
            k = ("d", i)
            if deps.get(k, 0) < self.dcnt[i]:
                deps[k] = self.dcnt[i]
        waits = self._waits(q, deps)
        self.dcnt[i] += 16
        tok = (("d", i), self.dcnt[i])
        self.ops[q].append((waits, lambda e: e.dma_start(out=out, in_=in_, **kw), tok[0], 16))
        self._mark(tok, reads, writes)
        return tok

    def async_op(self, q, fn, key, inc, reads=(), writes=()):
        if key not in self.sems:
            self.sems[key] = self.es.enter_context(self.nc.semaphore("a_%s" % str(key[1])))
            self.acnt[key] = 0
        deps = self._deps(reads, writes)
        waits = self._waits(q, deps)
        self.acnt[key] += inc
        tok = (key, self.acnt[key])
        self.ops[q].append((waits, fn, key, inc))
        self._mark(tok, reads, writes)
        return tok

    def wait_all(self, eng, toks):
        deps = {}
        for k, v in toks:
            if deps.get(k, 0) < v:
                deps[k] = v
        waits = self._waits(eng, deps)
        self.ops[eng].append((waits, None, None, 0))

    def emit(self):
        nc = self.nc
        sems = self.sems
        ops = self.ops

        def run(e, lst):
            for waits, fn, sk, inc in lst:
                for k, v in waits:
                    e.wait_ge(sems[k], v)
                if fn is not None:
                    ins = fn(e)
                    ins.then_inc(sems[sk], inc)

        with nc.Block() as block:

            @block.tensor
            def _(e):
                run(e, ops["pe"])

            @block.scalar
            def _(e):
                run(e, ops["act"])

            @block.vector
            def _(e):
                run(e, ops["dve"])

            @block.gpsimd
            def _(e):
                run(e, ops["pool"])

            @block.sync
            def _(e):
                run(e, ops["sp"])


C_G = 0
C_CW = 96
C_CB = 156
C_DTB = 168
C_AL = 200
C_DS = 232
C_PS = 248
C_FW = 256
C_FB = 388
C_PC = 432
C_V = 496
C_OH = 752
C_M = 768
NCST = 784


DBG_LAYOUT = {}


def build(debug=0):
    nc = bass.Bass("TRN2", target_bir_lowering=False)
    din = lambda n, s: nc.dram_tensor(n, s, F32, kind="ExternalInput").ap()
    xT_d = din("xT", [128, NB, TOK])
    xh_d = din("xhT", [128, NB, 16])
    pT_d = din("pT", [128, 2, TOK])
    cst_d = din("cst", [128, NCST])
    cmat_d = din("cmat", [128, 6, 128])
    nrm_d = din("nrm", [128, 1024])
    win_d = din("w_in", [7, 128, NB, 512])
    wdt_d = din("w_dt", [128, NB, 32])
    wout_d = din("w_out", [4, 128, NB, 512])
    pw_d = din("pool_w", [128, 4, 2, 256])
    wup_d = din("w_up", [NFB, 128, NB, 256])
    wdn_d = din("w_dn", [NB, 128, NFB, 128])
    wpg_d = din("w_pg", [8, 128, NB, 256])
    wpl_d = din("w_pl", [128, 2, D])
    out_d = nc.dram_tensor("out", [128, NB, TOK], F32, kind="ExternalOutput").ap()
    dbg_d = nc.dram_tensor("dbg", [128, debug], F32, kind="ExternalOutput").ap() if debug else None
    h1_dram = nc.dram_tensor("h1_spill", [128, NB, TOK], F32).ap()
    sz_dram = nc.dram_tensor("sz_spill", [128, 8, 1024], BF16).ap()
    ag1_in = nc.dram_tensor("ag1_in", [128, 2080], F32).ap()
    ag1_out = nc.dram_tensor("ag1_out", [NCORE * 128, 2080], F32).ap()
    ag2_in = nc.dram_tensor("ag2_in", [128, 32], F32).ap()
    ag2_out = nc.dram_tensor("ag2_out", [NCORE * 128, 32], F32).ap()

    es = ExitStack()
    with es:
        S = Sched(nc, es)
        CAP = 212000
        arena = es.enter_context(nc.sbuf_tensor("arena", [128, CAP // 4], F32))
        registry = []

        def view(off, shape, dt):
            n = int(np.prod(shape[1:]))
            nb = n * (2 if dt == BF16 else 4)
            assert off % 4 == 0 and nb % 4 == 0 and off + nb <= CAP, (off, nb, shape)
            a = arena[0:shape[0], off // 4:(off + nb) // 4]
            if dt != F32:
                a = a.bitcast(dt)
            if len(shape) == 3:
                a = a.rearrange("p (a b) -> p a b", b=shape[2])
            elif len(shape) == 4:
                a = a.rearrange("p (a b c) -> p a b c", b=shape[2], c=shape[3])
            return a, nb

        def mk(off, shape, dt, nbuf=1, name="t"):
            v, nb = view(off, shape, dt)
            bufs = S.bufs(nbuf, name)
            for (a, b, obufs) in registry:
                if a < off + nb and off < b:
                    for nbf in bufs:
                        S.inherit(nbf, obufs)
            registry.append((off, off + nb, bufs))
            return v, bufs

        class Bump:
            def __init__(self, base, cap):
                self.base, self.top, self.end = base, base, base + cap

            def __call__(self, shape, dt, nbuf=1, name="t"):
                n = int(np.prod(shape[1:]))
                nb = ((n * (2 if dt == BF16 else 4) + 63) // 64) * 64
                off = self.top
                self.top += nb
                assert self.top <= self.end, (name, self.top, self.end)
                return mk(off, shape, dt, nbuf, name)

            def reset(self):
                self.top = self.base

        PS = [es.enter_context(nc.psum_tensor("ps%d" % i, [128, 512], F32)) for i in range(7)]
        PSB = es.enter_context(nc.psum_tensor("psb", [128, 1024], BF16))
        bPS = S.bufs(7, "ps")
        bPSB = S.buf("psb")

        toks_out = []
        dbg_off = [0]

        def dump(name, ap, n, reads):
            if not debug:
                return
            o = dbg_off[0]
            dbg_off[0] += n
            assert dbg_off[0] <= debug, dbg_off[0]
            DBG_LAYOUT[name] = (o, n)
            toks_out.append(S.dma("sp", dbg_d[:, o:o + n], ap, reads=reads))

        Z0 = Bump(0, 12288)
        cst, (bcst,) = Z0([128, NCST], F32)
        cmat, _ = Z0([128, 6, 128], F32)
        cmatb, _ = Z0([128, 6, 128], BF16)
        nrm, _ = Z0([128, 1024], F32)
        S.dma("sp", cst, cst_d, writes=[bcst])
        S.dma("sp", cmat, cmat_d, writes=[bcst])
        S.dma("sp", nrm, nrm_d, writes=[bcst])
        S.op("act", lambda e: e.activation(out=cmatb, in_=cmat, func=AF.Copy), reads=[bcst], writes=[bcst])
        ident, Uincl, Lincl, onesf = cmat[:, 0, :], cmat[:, 1, :], cmat[:, 2, :], cmat[:, 5, :]
        identb, maskfb, maskbb, onesb = cmatb[:, 0, :], cmatb[:, 3, :], cmatb[:, 4, :], cmatb[:, 5, :]
        gain = lambda n, j: cst[:, C_G + n * 16 + j:C_G + n * 16 + j + 1]

        R_HN = 12288
        R_W = R_HN + 33280
        R_X = R_W + 33792
        R_S = R_X + 49152
        R_YP = R_S + 32768
        R_T = R_YP + 16384
        TP = Bump(R_T, 14336)
        TS = Bump(R_T + 14336, CAP - R_T - 14336)

        bc64 = lambda ap16: ap16.unsqueeze(2).to_broadcast([128, 16, 64])
        bc64h = lambda ap8: ap8.unsqueeze(2).to_broadcast([128, 8, 64])
        v3 = lambda ap: ap.rearrange("p (e q) -> p e q", q=64)
        r128 = lambda ap: ap.rearrange("p (a b) -> p a b", b=128)

        def rms_stats(src, n, src_bufs, sq, bsq, psbank, out_rstd, out_buf):
            for j in range(NB):
                S.op("act", lambda e, j=j: e.activation(out=sq[j % 2][:, 0:n], in_=src(j), func=AF.Square),
                     reads=[src_bufs[j]], writes=[bsq[j % 2]])
                S.op("pe", lambda e, j=j: e.matmul(PS[psbank][:, 0:n], lhsT=onesb, rhs=sq[j % 2][:, 0:n], start=(j == 0), stop=(j == NB - 1)),
                     reads=[bsq[j % 2], bcst], writes=[bPS[psbank]])
            S.op("act", lambda e: e.activation(out=out_rstd, in_=PS[psbank][:, 0:n], func=AF.Sqrt, scale=1.0 / D, bias=EPS),
                 reads=[bPS[psbank]], writes=[out_buf])
            S.op("dve", lambda e: e.reciprocal(out=out_rstd, in_=out_rstd), reads=[out_buf], writes=[out_buf])

        class WStream:
            def __init__(self, slots, srcs):
                self.slots, self.srcs, self.n, self.m = slots, srcs, 0, 0

            def prefetch(self):
                while self.n < len(self.srcs) and self.n < self.m + len(self.slots):
                    v, b = self.slots[self.n % len(self.slots)]
                    S.dma("pool", v, self.srcs[self.n], writes=[b])
                    self.n += 1

            def get(self):
                self.prefetch()
                sl = self.slots[self.m % len(self.slots)]
                self.m += 1
                return sl

        hnT, bhn_flat = mk(R_HN, [128, NB, TOK], BF16, 2 * NB, "hn")
        bhn = [bhn_flat[0:NB], bhn_flat[NB:2 * NB]]
        hnh, bhnh = mk(R_HN + 32768, [128, NB, 16], BF16, NB, "hnh")
        xh0, bxh0 = mk(R_X, [128, NB, 512], F32, NB, "xh0_")
        xh1, bxh1 = mk(R_S, [128, NB, 512], F32, NB, "xh1_")
        xh, bxh = [xh0, xh1], [bxh0, bxh1]
        xhalo, (bxhalo,) = TS([128, NB, 16], F32)
        rstd0, (brstd0,) = TS([128, 512], F32)
        rstd1, (brstd1,) = TS([128, 512], F32)
        rstd, brstd = [rstd0, rstd1], [brstd0, brstd1]
        rstdh, (brstdh,) = TS([128, 16], F32)
        sq0, (bsq0,) = TS([128, 512], BF16)
        sq1, (bsq1,) = TS([128, 512], BF16)
        sq, bsq = [sq0, sq1], [bsq0, bsq1]
        S.dma("sp", xhalo, xh_d, writes=[bxhalo])
        for h in range(2):
            for q in range(4):
                S.dma("sp", xh[h][:, q * 4:(q + 1) * 4, :], xT_d[:, q * 4:(q + 1) * 4, h * 512:(h + 1) * 512], writes=bxh[h][q * 4:(q + 1) * 4])
        ws0, (bws0,) = mk(R_W, [128, NB, 512], BF16, 1, "w0")
        ws1, (bws1,) = mk(R_W + 16384, [128, NB, 512], BF16, 1, "w1")
        wdt, (bwdt,) = mk(R_W + 32768, [128, NB, 32], BF16, 1, "wdt")
        WS = WStream([(ws0, bws0), (ws1, bws1)], [win_d[2], win_d[3], win_d[4], win_d[0], win_d[1], win_d[5], win_d[6]] + [wout_d[i] for i in range(4)])
        WS.prefetch()
        S.dma("pool", wdt, wdt_d, writes=[bwdt])
        for h in range(2):
            rms_stats(lambda j, h=h: xh[h][:, j, :], 512, bxh[h], sq, bsq, h, rstd[h], brstd[h])
            for j in range(NB):
                S.op("dve", lambda e, j=j, h=h: e.scalar_tensor_tensor(out=hnT[:, j, h * 512:(h + 1) * 512], in0=xh[h][:, j, :], scalar=gain(0, j),
                                                                      in1=rstd[h], op0=ALU.mult, op1=ALU.mult),
                     reads=[bxh[h][j], brstd[h], bcst], writes=[bhn[h][j]])
        rms_stats(lambda j: xhalo[:, j, :], 16, [bxhalo] * NB, sq, bsq, 2, rstdh, brstdh)
        for j in range(NB):
            S.op("dve", lambda e, j=j: e.scalar_tensor_tensor(out=hnh[:, j, :], in0=xhalo[:, j, :], scalar=gain(0, j), in1=rstdh,
                                                              op0=ALU.mult, op1=ALU.mult),
                 reads=[bxhalo, brstdh, bcst], writes=[bhnh[j]])
        dump("hn0", None, 0, []) if False else None

        xsT, bxs_flat = mk(R_X, [128, 8, TOK], F32, 64, "xs")
        bxs = [bxs_flat[j * 8:(j + 1) * 8] for j in range(8)]
        BfT, bBf = mk(R_X + 32768, [128, 2, TOK], F32, 2, "Bf")
        BbT, bBb = mk(R_X + 40960, [128, 2, TOK], BF16, 2, "Bb")
        CbT, bCb = mk(R_X + 45056, [128, 2, TOK], BF16, 2, "Cb")
        TS.reset()
        pre0, (bpre0,) = TS([128, 1040], F32)
        pre1, (bpre1,) = TS([128, 1040], F32)
        cacc0, (bcacc0,) = TS([128, 1024], F32)
        cacc1, (bcacc1,) = TS([128, 1024], F32)
        pre, bpre, cacc, bcacc = [pre0, pre1], [bpre0, bpre1], [cacc0, cacc1], [bcacc0, bcacc1]
        mcount = [0]

        def proj_fm_block(wv, wb, m, evac):
            q = mcount[0] % 2
            mcount[0] += 1
            banks = [3 * q, 3 * q + 1, 3 * q + 2]
            for h in range(2):
                for k in range(NB):
                    S.op("pe", lambda e, h=h, k=k: e.matmul(PS[banks[h]][:, 0:512], lhsT=wv[:, k, m * 128:(m + 1) * 128],
                                                            rhs=hnT[:, k, h * 512:(h + 1) * 512], start=(k == 0), stop=(k == NB - 1)),
                         reads=[wb, bhn[h][k]], writes=[bPS[banks[h]]])
            for k in range(NB):
                S.op("pe", lambda e, k=k: e.matmul(PS[banks[2]][:, 0:16], lhsT=wv[:, k, m * 128:(m + 1) * 128],
                                                   rhs=hnh[:, k, :], start=(k == 0), stop=(k == NB - 1)),
                     reads=[wb, bhnh[k]], writes=[bPS[banks[2]]])
            evac(banks)

        def evac_pre(banks, dst, bdst):
            S.op("act", lambda e: e.activation(out=dst[:, 8:520], in_=PS[banks[0]][:, 0:512], func=AF.Copy), reads=[bPS[banks[0]]], writes=[bdst])
            S.op("act", lambda e: e.activation(out=dst[:, 520:1032], in_=PS[banks[1]][:, 0:512], func=AF.Copy), reads=[bPS[banks[1]]], writes=[bdst])
            S.op("act", lambda e: e.activation(out=dst[:, 0:8], in_=PS[banks[2]][:, 0:8], func=AF.Copy), reads=[bPS[banks[2]]], writes=[bdst])
            S.op("act", lambda e: e.activation(out=dst[:, 1032:1040], in_=PS[banks[2]][:, 8:16], func=AF.Copy), reads=[bPS[banks[2]]], writes=[bdst])

        ccount = [0]

        def conv_block(ch, outs):
            def ev(banks):
                i = ccount[0] % 2
                ccount[0] += 1
                evac_pre(banks, pre[i], bpre[i])
                wcol = lambda t: cst[:, C_CW + ch * 5 + t:C_CW + ch * 5 + t + 1]
                S.op("dve", lambda e: e.tensor_scalar(out=cacc[i], in0=pre[i][:, 6:1030], scalar1=wcol(0), scalar2=None, op0=ALU.mult),
                     reads=[bpre[i], bcst], writes=[bcacc[i]])
                for t in range(1, 5):
                    S.op("dve", lambda e, t=t: e.scalar_tensor_tensor(out=cacc[i], in0=pre[i][:, 6 + t:1030 + t], scalar=wcol(t), in1=cacc[i],
                                                                      op0=ALU.mult, op1=ALU.add),
                         reads=[bpre[i], bcst, bcacc[i]], writes=[bcacc[i]])
                for dst, bl in outs:
                    S.op("act", lambda e, dst=dst: e.activation(out=dst, in_=cacc[i], func=AF.Silu, bias=cst[:, C_CB + ch:C_CB + ch + 1]),
                         reads=[bcacc[i], bcst], writes=bl)
            return ev

        for blk in range(2):
            wv, wb = WS.get()
            for m in range(4):
                j = blk * 4 + m
                proj_fm_block(wv, wb, m, conv_block(j, [(xsT[:, j, :], bxs[j])]))
        wv, wb = WS.get()
        for m in range(4):
            g = m % 2
            if m < 2:
                outs = [(BfT[:, g, :], [bBf[g]]), (BbT[:, g, :], [bBb[g]])]
            else:
                outs = [(CbT[:, g, :], [bCb[g]])]
            proj_fm_block(wv, wb, m, conv_block(8 + m, outs))
        if debug:
            for j in range(8):
                dump("xs%d" % j, xsT[:, j, :], 1024, bxs[j])
            dump("Bf0", BfT[:, 0, :], 1024, [bBf[0]])
            dump("Bf1", BfT[:, 1, :], 1024, [bBf[1]])

        TS.reset()
        szst0, (bszst0,) = TS([128, 512], BF16)
        szst1, (bszst1,) = TS([128, 512], BF16)
        szst, bszst = [szst0, szst1], [bszst0, bszst1]
        bszd = S.bufs(8, "szd")
        zc = [0]
        for blk in range(2):
            wv, wb = WS.get()
            for t in range(8):
                i = zc[0] % 2
                zc[0] += 1
                bank = 3 * (mcount[0] % 2)
                mcount[0] += 1
                for k in range(NB):
                    S.op("pe", lambda e, k=k, t=t, bank=bank, wv=wv: e.matmul(PS[bank][:, 0:512], lhsT=hnT[:, k, t * 128:(t + 1) * 128], rhs=wv[:, k, :],
                                                                              start=(k == 0), stop=(k == NB - 1)),
                         reads=[wb, bhn[t // 4][k]], writes=[bPS[bank]])
                S.op("act", lambda e, i=i, bank=bank: e.activation(out=szst[i], in_=PS[bank][:, 0:512], func=AF.Silu),
                     reads=[bPS[bank]], writes=[bszst[i]])
                S.dma("sp", sz_dram[:, t, blk * 512:(blk + 1) * 512], szst[i], reads=[bszst[i]], writes=[bszd[t]])
        dtraw, (bdt,) = TP([128, 8, 32], F32)
        dtv, (bdtv,) = TP([128, 8, 32], F32)
        dtt, (bdtt,) = TP([128, 8, 32], F32)
        da, (bda,) = TP([128, 8, 32], F32)
        nega, (bnega,) = TP([128, 32], F32)
        for t in range(8):
            for k in range(NB):
                S.op("pe", lambda e, k=k, t=t: e.matmul(PS[6][:, t * 32:(t + 1) * 32], lhsT=hnT[:, k, t * 128:(t + 1) * 128], rhs=wdt[:, k, :],
                                                        start=(k == 0), stop=(k == NB - 1)),
                     reads=[bwdt, bhn[t // 4][k]], writes=[bPS[6]])
        S.op("dve", lambda e: e.tensor_tensor(out=dtraw, in0=PS[6][:, 0:256].rearrange("p (c h) -> p c h", h=32),
                                              in1=cst[:, C_DTB:C_DTB + 32].unsqueeze(1).to_broadcast([128, 8, 32]), op=ALU.add),
             reads=[bPS[6], bcst], writes=[bdt])
        S.op("act", lambda e: e.activation(out=dtt, in_=dtraw, func=AF.Abs), reads=[bdt], writes=[bdtt])
        S.op("act", lambda e: e.activation(out=dtt, in_=dtt, func=AF.Exp, scale=-1.0), reads=[bdtt], writes=[bdtt])
        S.op("act", lambda e: e.activation(out=dtt, in_=dtt, func=AF.Ln, bias=1.0), reads=[bdtt], writes=[bdtt])
        S.op("dve", lambda e: e.tensor_scalar(out=dtv, in0=dtraw, scalar1=0.0, scalar2=None, op0=ALU.max), reads=[bdt], writes=[bdtv])
        S.op("dve", lambda e: e.tensor_tensor(out=dtv, in0=dtv, in1=dtt, op=ALU.add), reads=[bdtv, bdtt], writes=[bdtv])
        S.op("act", lambda e: e.activation(out=nega, in_=cst[:, C_AL:C_AL + 32], func=AF.Exp), reads=[bcst], writes=[bnega])
        S.op("dve", lambda e: e.tensor_scalar(out=nega, in0=nega, scalar1=-1.0, scalar2=None, op0=ALU.mult), reads=[bnega], writes=[bnega])
        S.op("dve", lambda e: e.tensor_tensor(out=da, in0=dtv, in1=nega.unsqueeze(1).to_broadcast([128, 8, 32]), op=ALU.mult),
             reads=[bdtv, bnega], writes=[bda])
        if debug:
            dump("dt", dtv.rearrange("p c h -> p (c h)"), 256, [bdtv])

        mixedT, bmixed = mk(R_S, [128, 8, TOK], BF16, 8, "mixed")
        poolw, (bpoolw,) = mk(R_S + 16384, [128, 4, 2, 256], BF16, 1, "poolw")
        ypoolT, byp_flat = mk(R_YP, [128, 8, TOK], BF16, 16, "ypool")
        bypool = [byp_flat[j * 2:(j + 1) * 2] for j in range(8)]
        S.dma("pool", poolw, pw_d, writes=[bpoolw])
        TS.reset()
        uext0, (buext0,) = TS([128, 1040], F32)
        uext1, (buext1,) = TS([128, 1040], F32)
        pta, (bpta,) = TS([128, 1040], F32)
        ptb, (bptb,) = TS([128, 1040], F32)
        uext, buext = [uext0, uext1], [buext0, buext1]
        ucount = [0]

        def pool_block(j):
            g = j // 2

            def ev(banks):
                i = ucount[0] % 2
                ucount[0] += 1
                u = uext[i]
                evac_pre(banks, u, buext[i])
                add = lambda o, a, b, rd, wr: S.op("dve", lambda e: e.tensor_tensor(out=o, in0=a, in1=b, op=ALU.add), reads=rd, writes=wr)
                add(pta[:, 1:1040], u[:, 0:1039], u[:, 1:1040], [buext[i]], [bpta])
                s, bs = pta, bpta
                if g >= 1:
                    add(ptb[:, 2:1039], pta[:, 1:1038], pta[:, 3:1040], [bpta], [bptb])
                    s, bs = ptb, bptb
                if g >= 2:
                    add(pta[:, 4:1037], ptb[:, 2:1035], ptb[:, 6:1039], [bptb], [bpta])
                    s, bs = pta, bpta
                if g >= 3:
                    add(ptb[:, 8:1033], pta[:, 4:1029], pta[:, 12:1037], [bpta], [bptb])
                    s, bs = ptb, bptb
                S.op("dve", lambda e: e.tensor_tensor(out=s[:, 8:16], in0=s[:, 8:16], in1=cst[:, C_PC + g * 16:C_PC + g * 16 + 8], op=ALU.mult),
                     reads=[bs, bcst], writes=[bs])
                S.op("dve", lambda e: e.tensor_tensor(out=s[:, 1024:1032], in0=s[:, 1024:1032], in1=cst[:, C_PC + g * 16 + 8:C_PC + g * 16 + 16], op=ALU.mult),
                     reads=[bs, bcst], writes=[bs])
                kk = float(2 ** (g + 1))
                S.op("dve", lambda e: e.scalar_tensor_tensor(out=mixedT[:, j, :], in0=s[:, 8:1032], scalar=1.0 / kk, in1=u[:, 8:1032],
                                                             op0=ALU.mult, op1=ALU.subtract),
                     reads=[bs, buext[i]], writes=[bmixed[j]])
            return ev

        def pool_mm(g):
            for dblk in range(2):
                for h in range(2):
                    bank = 3 * (mcount[0] % 2)
                    mcount[0] += 1
                    for kc in range(2):
                        S.op("pe", lambda e, kc=kc, bank=bank, h=h, dblk=dblk: e.matmul(
                            PS[bank][:, 0:512], lhsT=poolw[:, g, kc, dblk * 128:(dblk + 1) * 128], rhs=mixedT[:, 2 * g + kc, h * 512:(h + 1) * 512],
                            start=(kc == 0), stop=(kc == 1)),
                            reads=[bpoolw, bmixed[2 * g + kc]], writes=[bPS[bank]])
                    jj = 2 * g + dblk
                    S.op("act", lambda e, bank=bank, h=h, jj=jj: e.activation(out=ypoolT[:, jj, h * 512:(h + 1) * 512], in_=PS[bank][:, 0:512],
                                                                               func=AF.Identity, scale=cst[:, C_PS + jj:C_PS + jj + 1]),
                         reads=[bPS[bank], bcst], writes=[bypool[jj][h]])

        for blk in range(2):
            wv, wb = WS.get()
            for m in range(4):
                j = blk * 4 + m
                proj_fm_block(wv, wb, m, pool_block(j))
                if j % 2 == 1:
                    pool_mm(j // 2)
        if debug:
            dump("ypool0", None, 0, []) if False else None

        yT, byT_flat = mk(R_HN, [128, 8, TOK], BF16, 64, "yT")
        byT = [byT_flat[j * 8:(j + 1) * 8] for j in range(8)]
        Sst, bSst_flat = mk(R_S, [128, 8, 2, 1024], BF16, 16, "Sst")
        bSst = [bSst_flat[c * 2:(c + 1) * 2] for c in range(8)]
        T2 = Bump(R_HN + 16384, 16896)
        xs_tok, (bxstok,) = T2([128, 1024], F32)
        xdt, bxdt = [], []
        for q in range(4):
            v_, (b_,) = T2([128, 1024], BF16)
            xdt.append(v_)
            bxdt.append(b_)
        Btok, (bBtok,) = T2([128, 256], BF16)
        cbT, (bcbT,) = T2([128, 2, 128], F32)
        T3 = Bump(R_W, 16384)
        Ld0, (bLd0,) = T3([128, 4, 128], F32)
        Ld1, (bLd1,) = T3([128, 4, 128], F32)
        Mt0, (bMt0,) = T3([128, 4, 128], BF16)
        Mt1, (bMt1,) = T3([128, 4, 128], BF16)
        Mt2, (bMt2,) = T3([128, 4, 128], BF16)
        Mt3, (bMt3,) = T3([128, 4, 128], BF16)
        Ld, bLd, Mt, bMt = [Ld0, Ld1], [bLd0, bLd1], [Mt0, Mt1, Mt2, Mt3], [bMt0, bMt1, bMt2, bMt3]
        Hbf, bHbfd = T3([128, 2048], BF16, 2, "Hbf")
        tmpA, (btmpA,) = T3([128, 1024], F32)
        AR, bAR = TP([128, 8, 32], F32, 8, "AR")
        negAR, _ = TP([128, 8, 32], F32)
        wyo, _ = TP([128, 8, 32], F32)
        dtw, _ = TP([128, 8, 32], F32)
        cdb, _ = TP([128, 8, 32], F32)
        totb, _ = TP([128, 8, 32], F32)
        epb, _ = TP([128, 16], F32)
        pfx, (bpfx,) = TP([128, 16], F32)
        TS.reset()
        Ef, (bEf, bEb, bTk) = TS([128, 2080], F32, 3, "Ef")
        tmpB, (btmpB,) = TS([128, 1024], F32)
        for c in range(8):
            S.op("pe", lambda e, c=c: e.matmul(PS[6][:, 0:16], lhsT=Uincl, rhs=da[:, c, 0:16], start=True, stop=True), reads=[bcst, bda], writes=[bPS[6]])
            S.op("pe", lambda e, c=c: e.matmul(PS[6][:, 16:32], lhsT=Lincl, rhs=da[:, c, 16:32], start=True, stop=True), reads=[bcst, bda], writes=[bPS[6]])
            S.op("pe", lambda e, c=c: e.matmul(PS[6][:, 32:64], lhsT=onesf, rhs=da[:, c, :], start=True, stop=True), reads=[bcst, bda], writes=[bPS[6]])
            S.op("dve", lambda e, c=c: e.tensor_copy(out=AR[:, c, :], in_=PS[6][:, 0:32]), reads=[bPS[6]], writes=[bAR[c]])
            S.op("dve", lambda e, c=c: e.tensor_scalar(out=negAR[:, c, :], in0=PS[6][:, 0:32], scalar1=-1.0, scalar2=None, op0=ALU.mult), reads=[bPS[6]], writes=[bAR[c]])
            S.op("dve", lambda e, c=c: e.tensor_copy(out=totb[:, c, :], in_=PS[6][:, 32:64]), reads=[bPS[6]], writes=[bAR[c]])
            S.op("act", lambda e, c=c: e.activation(out=wyo[:, c, :], in_=PS[6][:, 0:32], func=AF.Exp), reads=[bPS[6]], writes=[bAR[c]])
            S.op("act", lambda e, c=c: e.activation(out=cdb[:, c, :], in_=PS[6][:, 32:64], func=AF.Exp), reads=[bPS[6]], writes=[bAR[c]])
            S.op("dve", lambda e, c=c: e.tensor_tensor(out=dtw[:, c, :], in0=totb[:, c, :], in1=AR[:, c, :], op=ALU.subtract), reads=[bAR[c]], writes=[bAR[c]])
            S.op("act", lambda e, c=c: e.activation(out=dtw[:, c, :], in_=dtw[:, c, :], func=AF.Exp), reads=[bAR[c]], writes=[bAR[c]])
            S.op("dve", lambda e, c=c: e.tensor_tensor(out=dtw[:, c, :], in0=dtw[:, c, :], in1=dtv[:, c, :], op=ALU.mult), reads=[bAR[c], bdtv], writes=[bAR[c]])
        S.op("dve", lambda e: e.memset(Ef[:, 0:2048], 0.0), writes=[bEf, bEb])
        S.op("dve", lambda e: e.memset(pfx, 0.0), writes=[bpfx])

        def ypart(c):
            return xsT[:, :, c * 128:(c + 1) * 128]

        bchunk = lambda c: [bxs[j][c] for j in range(8)]

        for c in range(8):
            cs = slice(c * 128, (c + 1) * 128)
            for j in range(8):
                S.op("pe", lambda e, j=j, cs=cs: e.transpose(PS[j // 4][:, (j % 4) * 128:(j % 4 + 1) * 128], xsT[:, j, cs], ident),
                     reads=[bxs[j][c], bcst], writes=[bPS[j // 4]])
            for g in range(2):
                S.op("pe", lambda e, g=g, cs=cs: e.transpose(PS[2][:, g * 128:(g + 1) * 128], BfT[:, g, cs], ident), reads=[bBf[g], bcst], writes=[bPS[2]])
            for hh in range(2):
                S.op("act", lambda e, hh=hh: e.activation(out=xs_tok[:, hh * 512:(hh + 1) * 512], in_=PS[hh][:, 0:512], func=AF.Copy),
                     reads=[bPS[hh]], writes=[bxstok])
            S.op("act", lambda e: e.activation(out=Btok, in_=PS[2][:, 0:256], func=AF.Copy), reads=[bPS[2]], writes=[bBtok])
            for q, (src, off) in enumerate([(dtv, 0), (dtv, 16), (dtw, 0), (dtw, 16)]):
                S.op("dve", lambda e, q=q, src=src, off=off, c=c: e.tensor_tensor(out=v3(xdt[q]), in0=v3(xs_tok), in1=bc64(src[:, c, off:off + 16]), op=ALU.mult),
                     reads=[bxstok, bdtv, bAR[c]], writes=[bxdt[q]])
            for g in range(2):
                S.op("pe", lambda e, g=g, cs=cs: e.matmul(PS[2][:, 256 + g * 128:256 + (g + 1) * 128], lhsT=BbT[:, g, cs], rhs=CbT[:, g, cs], start=True, stop=True),
                     reads=[bBb[g], bCb[g]], writes=[bPS[2]])
            S.op("act", lambda e: e.activation(out=cbT, in_=PS[2][:, 256:512].rearrange("p (g l) -> p g l", l=128), func=AF.Copy), reads=[bPS[2]], writes=[bcbT])
            gi = 0
            for g in range(2):
                for e4 in range(2):
                    pair = gi % 2
                    gi += 1
                    for d in range(2):
                        tri = Uincl if d == 0 else Lincl
                        mk_ = maskfb if d == 0 else maskbb
                        i = d
                        mi = pair * 2 + d
                        sb_ = 3 if d == 0 else 6
                        for q in range(4):
                            hd = d * 16 + g * 8 + e4 * 4 + q
                            S.op("pe", lambda e, q=q, hd=hd, sb_=sb_, tri=tri, c=c: e.matmul(PS[sb_][:, q * 128:(q + 1) * 128], lhsT=da[:, c, hd:hd + 1].to_broadcast([128, 128]),
                                                                                          rhs=tri, start=True, stop=False),
                                 reads=[bda, bcst], writes=[bPS[sb_]])
                            S.op("pe", lambda e, q=q, sb_=sb_, mk_=mk_: e.matmul(PS[sb_][:, q * 128:(q + 1) * 128], lhsT=identb, rhs=mk_, start=False, stop=True),
                                 reads=[bcst], writes=[bPS[sb_]])
                        for q in range(4):
                            hd = d * 16 + g * 8 + e4 * 4 + q
                            S.op("act", lambda e, q=q, hd=hd, sb_=sb_, i=i, c=c: e.activation(out=Ld[i][:, q, :], in_=PS[sb_][:, q * 128:(q + 1) * 128], func=AF.Exp,
                                                                                           bias=negAR[:, c, hd:hd + 1]),
                                 reads=[bPS[sb_], bAR[c]], writes=[bLd[i]])
                        S.op("dve", lambda e, i=i, mi=mi, g=g: e.tensor_tensor(out=Mt[mi], in0=Ld[i], in1=cbT[:, g, :].unsqueeze(1).to_broadcast([128, 4, 128]), op=ALU.mult),
                             reads=[bLd[i], bcbT], writes=[bMt[mi]])
                    for q in range(4):
                        eh = e4 * 4 + q
                        for d in range(2):
                            mi = pair * 2 + d
                            S.op("pe", lambda e, q=q, mi=mi, g=g, d=d, eh=eh: e.matmul(PS[4 + g][:, eh * 64:(eh + 1) * 64], lhsT=Mt[mi][:, q, :],
                                                                                     rhs=xdt[d][:, g * 512 + eh * 64:g * 512 + (eh + 1) * 64], start=(d == 0), stop=(d == 1)),
                                 reads=[bMt[mi], bxdt[d]], writes=[bPS[4 + g]])
            S.op("dve", lambda e: e.tensor_tensor(out=v3(tmpA), in0=v3(xs_tok), in1=bc64(cst[:, C_DS:C_DS + 16]), op=ALU.mult), reads=[bxstok, bcst], writes=[btmpA])
            for g in range(2):
                S.op("dve", lambda e, g=g, c=c: e.tensor_tensor(out=ypart(c)[:, g * 4:(g + 1) * 4, :], in0=r128(PS[4 + g][:, 0:512]),
                                                              in1=r128(tmpA[:, g * 512:(g + 1) * 512]), op=ALU.add),
                     reads=[bPS[4 + g], btmpA], writes=bchunk(c)[g * 4:(g + 1) * 4])
            for d in range(2):
                for g in range(2):
                    S.op("pe", lambda e, d=d, g=g: e.matmul(PS[g][:, 0:512], lhsT=Btok[:, g * 128:(g + 1) * 128], rhs=xdt[2 + d][:, g * 512:(g + 1) * 512], start=True, stop=True),
                         reads=[bBtok, bxdt[2 + d]], writes=[bPS[g]])
                for g in range(2):
                    S.op("act", lambda e, d=d, g=g, c=c: e.activation(out=Sst[:, c, d, g * 512:(g + 1) * 512], in_=PS[g][:, 0:512], func=AF.Copy),
                         reads=[bPS[g]], writes=[bSst[c][d]])
                if d == 0:
                    S.op("dve", lambda e, c=c: e.tensor_tensor(out=v3(Ef[:, 0:1024]), in0=v3(Ef[:, 0:1024]), in1=bc64(cdb[:, c, 0:16]), op=ALU.mult),
                         reads=[bEf, bAR[c]], writes=[bEf])
                    for g in range(2):
                        S.op("dve", lambda e, g=g: e.tensor_tensor(out=Ef[:, g * 512:(g + 1) * 512], in0=Ef[:, g * 512:(g + 1) * 512], in1=PS[g][:, 0:512], op=ALU.add),
                             reads=[bEf, bPS[g]], writes=[bEf])
                else:
                    S.op("act", lambda e: e.activation(out=epb, in_=pfx, func=AF.Exp), reads=[bpfx], writes=[bpfx])
                    for g in range(2):
                        S.op("dve", lambda e, g=g: e.tensor_tensor(out=v3(tmpB)[:, g * 8:(g + 1) * 8, :], in0=v3(PS[g][:, 0:512]),
                                                                   in1=bc64h(epb[:, g * 8:(g + 1) * 8]), op=ALU.mult),
                             reads=[bPS[g], bpfx], writes=[btmpB])
                    S.op("dve", lambda e: e.tensor_tensor(out=Ef[:, 1024:2048], in0=Ef[:, 1024:2048], in1=tmpB, op=ALU.add), reads=[bEb, btmpB], writes=[bEb])
                    S.op("dve", lambda e, c=c: e.tensor_tensor(out=pfx, in0=pfx, in1=totb[:, c, 16:32], op=ALU.add), reads=[bpfx, bAR[c]], writes=[bpfx])
        S.op("dve", lambda e: e.tensor_reduce(out=Ef[:, 2048:2080], in_=totb.rearrange("p c h -> p h c"), axis=AX.X, op=ALU.add), reads=bAR, writes=[bTk])
        if debug:
            for c in range(8):
                S.op("dve", lambda e, c=c: e.tensor_copy(out=r128(tmpB), in_=ypart(c)), reads=bchunk(c), writes=[btmpB])
                dump("ypart%d" % c, tmpB, 1024, [btmpB])
            dump("Ef", Ef, 2080, [bEf, bEb, bTk])

        bag1i, bag1o = S.buf(), S.buf()
        S.dma("sp", ag1_in, Ef, reads=[bEf, bEb, bTk], writes=[bag1i])
        S.async_op("pool", lambda e: e.collective_compute("AllGather", ALU.bypass, replica_groups=[list(range(NCORE))],
                                                          ins=[ag1_in.opt()], outs=[ag1_out.opt()]),
                   ("c", 1), 1, reads=[bag1i], writes=[bag1o])
        T2b = Bump(R_HN + 16384, 16896)
        Ej0, (bEj0,) = T2b([128, 2080], F32)
        Ej1, (bEj1,) = T2b([128, 2080], F32)
        Ej, bEj = [Ej0, Ej1], [bEj0, bEj1]
        Tall, (bTall,) = TP([8, 32], F32)
        rhsM, (brhsM,) = TP([8, 8, 32], F32)
        coef, (bcoef,) = TP([128, 8, 32], F32)
        gss, (bgss,) = TP([128, 2], F32)
        yn, (byn,) = TS([128, 1024], BF16)
        junk, (bjunk,) = TS([128, 512], BF16)
        szr0, (bszr0,) = TS([128, 1024], BF16)
        szr1, (bszr1,) = TS([128, 1024], BF16)
        szr, bszr = [szr0, szr1], [bszr0, bszr1]
        Hs, bHd = Ef, [bEf, bEb]
        ag1v = ag1_out.rearrange("(r p) f -> p r f", p=128)
        S.dma("sp", Tall, ag1v[0:1, :, 2048:2080].rearrange("o r f -> (o r) f"), reads=[bag1o], writes=[bTall])
        for d in range(2):
            S.op("dve", lambda e, d=d: e.tensor_tensor(out=rhsM[:, :, d * 16:(d + 1) * 16], in0=cst[0:8, C_M + d * 8:C_M + (d + 1) * 8].unsqueeze(2).to_broadcast([8, 8, 16]),
                                                       in1=Tall[:, d * 16:(d + 1) * 16].unsqueeze(1).to_broadcast([8, 8, 16]), op=ALU.mult),
                 reads=[bTall, bcst], writes=[brhsM])
        S.op("pe", lambda e: e.matmul(PS[6][:, 0:256], lhsT=onesf[0:8, :], rhs=rhsM.rearrange("p j e -> p (j e)"), start=True, stop=True),
             reads=[brhsM, bcst], writes=[bPS[6]])
        cf2 = coef.rearrange("p j e -> p (j e)")
        S.op("act", lambda e: e.activation(out=cf2, in_=PS[6][:, 0:256], func=AF.Exp), reads=[bPS[6]], writes=[bcoef])
        S.op("dve", lambda e: e.tensor_tensor(out=cf2, in0=cf2, in1=cst[:, C_V:C_V + 256], op=ALU.mult), reads=[bcoef, bcst], writes=[bcoef])
        S.op("dve", lambda e: e.memset(Hs[:, 0:2048], 0.0), writes=bHd)
        for j in range(NCORE):
            i = j % 2
            S.dma("sp", Ej[i][:, 0:2048], ag1v[:, j, 0:2048], reads=[bag1o], writes=[bEj[i]])
            for d in range(2):
                eng_ = "dve"
                tm_, btm_ = (tmpA, btmpA) if d == 0 else (tmpB, btmpB)
                S.op(eng_, lambda e, d=d, i=i, j=j, tm_=tm_: e.tensor_tensor(out=v3(tm_), in0=v3(Ej[i][:, d * 1024:(d + 1) * 1024]), in1=bc64(coef[:, j, d * 16:(d + 1) * 16]), op=ALU.mult),
                     reads=[bEj[i], bcoef], writes=[btm_])
                S.op(eng_, lambda e, d=d, tm_=tm_: e.tensor_tensor(out=Hs[:, d * 1024:(d + 1) * 1024], in0=Hs[:, d * 1024:(d + 1) * 1024], in1=tm_, op=ALU.add),
                     reads=[btm_, bHd[d]], writes=[bHd[d]])
        if debug:
            dump("Hin", Hs[:, 0:2048], 2048, bHd)
        for step in range(8):
            for d in range(2):
                c = step if d == 0 else 7 - step
                cs = slice(c * 128, (c + 1) * 128)
                S.op("act", lambda e, d=d: e.activation(out=Hbf[:, d * 1024:(d + 1) * 1024], in_=Hs[:, d * 1024:(d + 1) * 1024], func=AF.Copy), reads=[bHd[d]], writes=[bHbfd[d]])
                for g in range(2):
                    bank = 2 * d + g
                    S.op("pe", lambda e, d=d, g=g, cs=cs, bank=bank: e.matmul(PS[bank][:, 0:512], lhsT=CbT[:, g, cs], rhs=Hbf[:, d * 1024 + g * 512:d * 1024 + (g + 1) * 512], start=True, stop=True),
                         reads=[bCb[g], bHbfd[d]], writes=[bPS[bank]])
                    S.op("dve", lambda e, d=d, g=g, c=c, bank=bank: e.tensor_tensor(out=v3(tmpB)[:, g * 8:(g + 1) * 8, :], in0=v3(PS[bank][:, 0:512]),
                                                                                  in1=bc64h(wyo[:, c, d * 16 + g * 8:d * 16 + (g + 1) * 8]), op=ALU.mult),
                         reads=[bPS[bank], bAR[c]], writes=[btmpB])
                S.op("dve", lambda e, c=c: e.tensor_tensor(out=ypart(c), in0=ypart(c), in1=r128(tmpB), op=ALU.add),
                     reads=[btmpB] + bchunk(c), writes=bchunk(c))
                if step < 7:
                    S.op("dve", lambda e, d=d, c=c: e.tensor_tensor(out=v3(Hs[:, d * 1024:(d + 1) * 1024]), in0=v3(Hs[:, d * 1024:(d + 1) * 1024]), in1=bc64(cdb[:, c, d * 16:(d + 1) * 16]), op=ALU.mult),
                         reads=[bHd[d], bAR[c]], writes=[bHd[d]])
                    S.op("dve", lambda e, d=d, c=c: e.tensor_tensor(out=Hs[:, d * 1024:(d + 1) * 1024], in0=Hs[:, d * 1024:(d + 1) * 1024], in1=Sst[:, c, d, :], op=ALU.add),
                         reads=[bHd[d], bSst[c][d]], writes=[bHd[d]])
        for c in range(8):
            i = c % 2
            if debug:
                S.op("dve", lambda e, c=c: e.tensor_copy(out=r128(tmpB), in_=ypart(c)), reads=bchunk(c), writes=[btmpB])
                dump("yfin%d" % c, tmpB, 1024, [btmpB])
            S.dma("sp", szr[i], sz_dram[:, c, :], reads=[bszd[c]], writes=[bszr[i]])
            S.op("dve", lambda e, c=c, i=i: e.tensor_tensor(out=r128(tmpA), in0=ypart(c), in1=r128(szr[i]), op=ALU.mult),
                 reads=bchunk(c) + [bszr[i]], writes=[btmpA])
            for g in range(2):
                S.op("act", lambda e, g=g: e.activation(out=junk, in_=tmpA[:, g * 512:(g + 1) * 512], func=AF.Square, accum_out=gss[:, g:g + 1]), reads=[btmpA], writes=[bjunk, bgss])
            S.op("act", lambda e: e.activation(out=gss, in_=gss, func=AF.Sqrt, scale=1.0 / 512, bias=EPS), reads=[bgss], writes=[bgss])
            S.op("dve", lambda e: e.reciprocal(out=gss, in_=gss), reads=[bgss], writes=[bgss])
            for g in range(2):
                S.op("dve", lambda e, g=g: e.scalar_tensor_tensor(out=yn[:, g * 512:(g + 1) * 512], in0=tmpA[:, g * 512:(g + 1) * 512], scalar=gss[:, g:g + 1],
                                                                  in1=nrm[:, g * 512:(g + 1) * 512], op0=ALU.mult, op1=ALU.mult),
                     reads=[btmpA, bgss, bcst], writes=[byn])
            for j in range(8):
                S.op("pe", lambda e, j=j: e.transpose(PSB[:, j * 128:(j + 1) * 128], yn[:, j * 128:(j + 1) * 128], identb), reads=[byn, bcst], writes=[bPSB])
            S.op("act", lambda e, c=c: e.activation(out=yT[:, :, c * 128:(c + 1) * 128], in_=r128(PSB[:, 0:1024]), func=AF.Copy),
                 reads=[bPSB], writes=[byT[j][c] for j in range(8)])
        if debug:
            pass

        S.inherit(bws0, [bLd0, bLd1, bMt0, bMt1, bMt2, bMt3, btmpA] + bHbfd)
        hT, bh_flat = mk(R_X, [128, NB, TOK], F32, 2 * NB, "h")
        bh = [bh_flat[f * 2:(f + 1) * 2] for f in range(NB)]
        for blk in range(4):
            wv, wb = WS.get()
            for m in range(4):
                f = blk * 4 + m
                for h in range(2):
                    bank = mcount[0] % 6
                    mcount[0] += 1
                    for k in range(NB):
                        if k < 8:
                            rhs, rb = yT[:, k, h * 512:(h + 1) * 512], [byT[k][cc] for cc in range(h * 4, h * 4 + 4)]
                        else:
                            rhs, rb = ypoolT[:, k - 8, h * 512:(h + 1) * 512], [bypool[k - 8][h]]
                        S.op("pe", lambda e, k=k, rhs=rhs, bank=bank, m=m, wv=wv: e.matmul(PS[bank][:, 0:512], lhsT=wv[:, k, m * 128:(m + 1) * 128], rhs=rhs,
                                                                                       start=(k == 0), stop=(k == NB - 1)),
                             reads=[wb] + rb, writes=[bPS[bank]])
                    S.op("act", lambda e, f=f, h=h, bank=bank: e.activation(out=hT[:, f, h * 512:(h + 1) * 512], in_=PS[bank][:, 0:512], func=AF.Copy),
                         reads=[bPS[bank]], writes=[bh[f][h]])
        if debug:
            for f in range(NB):
                dump("mix%d" % f, hT[:, f, :], 1024, bh[f])

        def norm_tmps(alloc):
            rs0, (brs0,) = alloc([128, 512], F32)
            rs1, (brs1,) = alloc([128, 512], F32)
            res, bres = [], []
            for i in range(3):
                v_, (b_,) = alloc([128, 512], F32)
                res.append(v_)
                bres.append(b_)
            q0, (bq0,) = alloc([128, 512], BF16)
            q1, (bq1,) = alloc([128, 512], BF16)
            return dict(rs=[rs0, rs1], brs=[brs0, brs1], res=res, bres=bres, sq=[q0, q1], bsq=[bq0, bq1])

        rcount = [0]

        def post_norm_residual(NT, src, bsrc, gi_post, resid_dram, resid_reads, dst, bdst, final_out=None, halves=(0, 1), scol=lambda h: slice(h * 512, (h + 1) * 512)):
            for h in halves:
                rms_stats(lambda j, h=h: src[:, j, scol(h)], 512, [bsrc[j][h] for j in range(NB)], NT["sq"], NT["bsq"], 6, NT["rs"][h], NT["brs"][h])
                for f in range(NB):
                    i = rcount[0] % 3
                    rcount[0] += 1
                    r_, br_ = NT["res"][i], NT["bres"][i]
                    if resid_dram is not None:
                        S.dma("sp", r_, resid_dram[:, f, h * 512:(h + 1) * 512], reads=resid_reads(f, h), writes=[br_])
                        S.op("dve", lambda e, f=f, h=h: e.scalar_tensor_tensor(out=dst[:, f, h * 512:(h + 1) * 512], in0=src[:, f, scol(h)], scalar=gain(gi_post, f),
                                                                              in1=NT["rs"][h], op0=ALU.mult, op1=ALU.mult),
                             reads=[bsrc[f][h], NT["brs"][h], bcst], writes=[bdst[f][h]])
                        S.op("dve", lambda e, f=f, h=h, r_=r_: e.tensor_tensor(out=dst[:, f, h * 512:(h + 1) * 512], in0=dst[:, f, h * 512:(h + 1) * 512], in1=r_, op=ALU.add),
                             reads=[bdst[f][h], br_], writes=[bdst[f][h]])
                    else:
                        S.op("dve", lambda e, f=f, h=h, r_=r_: e.scalar_tensor_tensor(out=r_, in0=src[:, f, scol(h)], scalar=gain(gi_post, f),
                                                                                     in1=NT["rs"][h], op0=ALU.mult, op1=ALU.mult),
                             reads=[bsrc[f][h], NT["brs"][h], bcst], writes=[br_])
                        S.op("dve", lambda e, f=f, h=h, r_=r_: e.tensor_tensor(out=r_, in0=r_, in1=dst[:, f, h * 512:(h + 1) * 512], op=ALU.add),
                             reads=[bdst[f][h], br_], writes=[br_])
                        toks_out.append(S.dma("sp", final_out[:, f, h * 512:(h + 1) * 512], r_, reads=[br_]))

        def pre_norm(NT, src, bsrc, gi_pre, dst, bdst, halves=(0, 1)):
            for h in halves:
                rms_stats(lambda j, h=h: src[:, j, h * 512:(h + 1) * 512], 512, [bsrc[j][h] for j in range(NB)], NT["sq"], NT["bsq"], 6, NT["rs"][h], NT["brs"][h])
                for f in range(NB):
                    S.op("dve", lambda e, f=f, h=h: e.scalar_tensor_tensor(out=dst[:, f, h * 512:(h + 1) * 512], in0=src[:, f, h * 512:(h + 1) * 512], scalar=gain(gi_pre, f),
                                                                          in1=NT["rs"][h], op0=ALU.mult, op1=ALU.mult),
                         reads=[bsrc[f][h], NT["brs"][h], bcst], writes=[bdst[h][f]])

        TS.reset()
        NT = norm_tmps(TS)
        bspill = [S.bufs(2, "sp%d_" % f) for f in range(NB)]
        hn2, bhn2_flat = mk(R_HN, [128, NB, TOK], BF16, 2 * NB, "hn2")
        bhn2 = [bhn2_flat[0:NB], bhn2_flat[NB:2 * NB]]
        hhalo, (bhhalo,) = mk(R_HN + 32768, [128, NB, 2], BF16, 1, "hhalo")
        for hh in range(2):
            post_norm_residual(NT, hT, bh, 1, xT_d, lambda f, h: [], hT, bh, halves=(hh,))
            for f in range(NB):
                S.dma("sp", h1_dram[:, f, hh * 512:(hh + 1) * 512], hT[:, f, hh * 512:(hh + 1) * 512], reads=[bh[f][hh]], writes=[bspill[f][hh]])
            pre_norm(NT, hT, bh, 2, hn2, bhn2, halves=(hh,))
        if debug:
            for f in range(NB):
                dump("h1_%d" % f, hT[:, f, :], 1024, bh[f])

        actT, bact_flat = mk(R_X, [128, NFB, TOK], BF16, 2 * NFB, "act")
        bact = [bact_flat[j * 2:(j + 1) * 2] for j in range(NFB)]
        G0 = R_X + NFB * TOK * 2
        TG = Bump(G0, CAP - G0)
        gpre0, (bgpre0,) = TG([128, 1026], F32)
        gpre1, (bgpre1,) = TG([128, 1026], F32)
        gacc0, (bgacc0,) = TG([128, 1024], F32)
        gacc1, (bgacc1,) = TG([128, 1024], F32)
        gpre, bgpre, gacc, bgacc = [gpre0, gpre1], [bgpre0, bgpre1], [gacc0, gacc1], [bgacc0, bgacc1]
        gmark = TG.top
        bnd, (bbnd,) = TG([128, NB, 2], F32)
        gat, (bgat,) = TG([128, 8, 32], F32)
        selt, (bselt,) = TG([128, 8, 16], F32)
        hsel, (bhsel,) = TG([128, NB, 2], F32)
        wds0, (bwd0,) = TG([128, NFB, 128], BF16)
        wds1, (bwd1,) = TG([128, NFB, 128], BF16)
        S.op("dve", lambda e: e.tensor_copy(out=bnd[:, :, 0:1], in_=hn2[:, :, 0:1]), reads=bhn2[0], writes=[bbnd])
        S.op("dve", lambda e: e.tensor_copy(out=bnd[:, :, 1:2], in_=hn2[:, :, TOK - 1:TOK]), reads=bhn2[1], writes=[bbnd])
        bag2i, bag2o = S.buf(), S.buf()
        S.dma("sp", ag2_in, bnd.rearrange("p f t -> p (f t)"), reads=[bbnd], writes=[bag2i])
        S.async_op("pool", lambda e: e.collective_compute("AllGather", ALU.bypass, replica_groups=[list(range(NCORE))],
                                                          ins=[ag2_in.opt()], outs=[ag2_out.opt()]),
                   ("c", 2), 1, reads=[bag2i], writes=[bag2o])
        S.dma("sp", gat, ag2_out.rearrange("(r p) f -> p r f", p=128), reads=[bag2o], writes=[bgat])
        gat4 = gat.rearrange("p r (f t) -> p r f t", t=2)
        for side in range(2):
            S.op("dve", lambda e, side=side: e.tensor_tensor(out=selt, in0=gat4[:, :, :, 1 - side], in1=cst[:, C_OH + side * 8:C_OH + (side + 1) * 8].unsqueeze(2).to_broadcast([128, 8, 16]), op=ALU.mult),
                 reads=[bgat, bcst], writes=[bselt])
            S.op("dve", lambda e, side=side: e.tensor_reduce(out=hsel[:, :, side], in_=selt.rearrange("p r f -> p f r"), axis=AX.X, op=ALU.add), reads=[bselt], writes=[bhsel])
        S.op("dve", lambda e: e.tensor_copy(out=hhalo, in_=hsel), reads=[bhsel], writes=[bhhalo])
        wu_slots = []
        for i in range(3):
            v_, (b_,) = mk(R_W + i * 8192, [128, NB, 256], BF16, 1, "wu")
            wu_slots.append((v_, b_))
        WU = WStream(wu_slots, [wup_d[j] for j in range(NFB)])
        WD = WStream([(wds0, bwd0), (wds1, bwd1)], [wdn_d[m] for m in range(NB)])
        for j in range(NFB):
            wv, wb = WU.get()
            q = j % 2
            gb = [2 * q, 2 * q + 1]
            for h in range(2):
                for k in range(NB):
                    S.op("pe", lambda e, h=h, k=k, wv=wv, gb=gb: e.matmul(PS[gb[h]][:, 0:512], lhsT=wv[:, k, 0:128], rhs=hn2[:, k, h * 512:(h + 1) * 512], start=(k == 0), stop=(k == NB - 1)),
                         reads=[wb, bhn2[h][k]], writes=[bPS[gb[h]]])
            hc = (j % 128) * 2
            for k in range(NB):
                S.op("pe", lambda e, k=k, wv=wv, hc=hc: e.matmul(PS[6][:, hc:hc + 2], lhsT=wv[:, k, 0:128], rhs=hhalo[:, k, :], start=(k == 0), stop=(k == NB - 1)),
                     reads=[wb, bhhalo], writes=[bPS[6]])
            for h in range(2):
                for k in range(NB):
                    S.op("pe", lambda e, h=h, k=k, wv=wv: e.matmul(PS[4 + h][:, 0:512], lhsT=wv[:, k, 128:256], rhs=hn2[:, k, h * 512:(h + 1) * 512], start=(k == 0), stop=(k == NB - 1)),
                         reads=[wb, bhn2[h][k]], writes=[bPS[4 + h]])
            gp, ga, bgp, bga = gpre[q], gacc[q], bgpre[q], bgacc[q]
            S.op("act", lambda e, gp=gp, gb=gb: e.activation(out=gp[:, 1:513], in_=PS[gb[0]][:, 0:512], func=AF.Copy), reads=[bPS[gb[0]]], writes=[bgp])
            S.op("act", lambda e, gp=gp, gb=gb: e.activation(out=gp[:, 513:1025], in_=PS[gb[1]][:, 0:512], func=AF.Copy), reads=[bPS[gb[1]]], writes=[bgp])
            S.op("act", lambda e, gp=gp, hc=hc: e.activation(out=gp[:, 0:1], in_=PS[6][:, hc:hc + 1], func=AF.Copy), reads=[bPS[6]], writes=[bgp])
            S.op("act", lambda e, gp=gp, hc=hc: e.activation(out=gp[:, 1025:1026], in_=PS[6][:, hc + 1:hc + 2], func=AF.Copy), reads=[bPS[6]], writes=[bgp])
            fw = lambda t, j=j: cst[:, C_FW + j * 3 + t:C_FW + j * 3 + t + 1]
            S.op("act", lambda e, gp=gp, ga=ga, j=j, fw=fw: e.activation(out=ga, in_=gp[:, 1:1025], func=AF.Identity, scale=fw(1), bias=cst[:, C_FB + j:C_FB + j + 1]),
                 reads=[bgp, bcst], writes=[bga])
            S.op("dve", lambda e, gp=gp, ga=ga, fw=fw: e.scalar_tensor_tensor(out=ga, in0=gp[:, 0:1024], scalar=fw(0), in1=ga, op0=ALU.mult, op1=ALU.add),
                 reads=[bgp, bga, bcst], writes=[bga])
            S.op("dve", lambda e, gp=gp, ga=ga, fw=fw: e.scalar_tensor_tensor(out=ga, in0=gp[:, 2:1026], scalar=fw(2), in1=ga, op0=ALU.mult, op1=ALU.add),
                 reads=[bgp, bga, bcst], writes=[bga])
            S.op("act", lambda e, ga=ga: e.activation(out=ga, in_=ga, func=AF.Gelu_apprx_tanh), reads=[bga], writes=[bga])
            for h in range(2):
                S.op("dve", lambda e, h=h, ga=ga, j=j: e.tensor_tensor(out=actT[:, j, h * 512:(h + 1) * 512], in0=ga[:, h * 512:(h + 1) * 512], in1=PS[4 + h][:, 0:512], op=ALU.mult),
                     reads=[bga, bPS[4 + h]], writes=[bact[j][h]])
            if j == NFB - 2:
                WD.prefetch()
        if debug:
            for j in range(0, NFB, 11):
                S.op("act", lambda e, j=j: e.activation(out=gacc[0], in_=actT[:, j, :], func=AF.Copy), reads=bact[j], writes=[bgacc[0]])
                dump("act%d" % j, gacc[0], 1024, [bgacc[0]])
        fT, bf_flat = mk(R_HN, [128, NB, TOK], F32, 2 * NB, "ff")
        bff = [bf_flat[f * 2:(f + 1) * 2] for f in range(NB)]
        for m in range(NB):
            wv, wb = WD.get()
            for h in range(2):
                bank = (2 * m + h) % 6
                for k in range(NFB):
                    S.op("pe", lambda e, k=k, h=h, wv=wv, bank=bank: e.matmul(PS[bank][:, 0:512], lhsT=wv[:, k, :], rhs=actT[:, k, h * 512:(h + 1) * 512], start=(k == 0), stop=(k == NFB - 1)),
                         reads=[wb, bact[k][h]], writes=[bPS[bank]])
                S.op("act", lambda e, m=m, h=h, bank=bank: e.activation(out=fT[:, m, h * 512:(h + 1) * 512], in_=PS[bank][:, 0:512], func=AF.Copy),
                     reads=[bPS[bank]], writes=[bff[m][h]])
        TG.top = G0
        NT2 = norm_tmps(TG)
        hn3, bhn3_flat = mk(R_X, [128, NB, TOK], BF16, 2 * NB, "hn3")
        bhn3 = [bhn3_flat[0:NB], bhn3_flat[NB:2 * NB]]
        for hh in range(2):
            post_norm_residual(NT2, fT, bff, 3, h1_dram, lambda f, h: [bspill[f][h]], fT, bff, halves=(hh,))
            pre_norm(NT2, fT, bff, 4, hn3, bhn3, halves=(hh,))
        if debug:
            for f in range(NB):
                dump("h2_%d" % f, fT[:, f, :], 1024, bff[f])
        tTh, bt_flat = mk(R_X + 33280, [128, NB, 512], F32, NB, "tT")
        bt = [[bt_flat[f], bt_flat[f]] for f in range(NB)]
        pTb, (bpT,) = mk(R_X + 66048, [128, 2, TOK], BF16, 1, "pT")
        wpl, (bwpl,) = mk(R_X + 70144, [128, 2, D], BF16, 1, "wpl")
        S.dma("pool", pTb, pT_d, writes=[bpT])
        S.dma("pool", wpl, wpl_d, writes=[bwpl])
        TH = Bump(TG.top, CAP - TG.top)
        wg_slots = []
        for i in range(2):
            v_, (b_,) = TH([128, NB, 256], BF16)
            wg_slots.append((v_, b_))
        sg0, (bsg0,) = TH([128, 512], F32)
        sg1, (bsg1,) = TH([128, 512], F32)
        sg, bsg = [sg0, sg1], [bsg0, bsg1]
        WG = WStream(wg_slots, [wpg_d[b] for b in range(8)] * 2)
        gc = [0]
        for h in range(2):
            for blk in range(8):
                wv, wb = WG.get()
                for m in range(2):
                    f = blk * 2 + m
                    q = gc[0] % 2
                    gc[0] += 1
                    gbk, pbk = 2 * q, 2 * q + 1
                    for k in range(NB):
                        S.op("pe", lambda e, k=k, h=h, m=m, wv=wv, gbk=gbk: e.matmul(PS[gbk][:, 0:512], lhsT=wv[:, k, m * 128:(m + 1) * 128], rhs=hn3[:, k, h * 512:(h + 1) * 512],
                                                                                 start=(k == 0), stop=(k == NB - 1)),
                             reads=[wb, bhn3[h][k]], writes=[bPS[gbk]])
                    for kc in range(2):
                        S.op("pe", lambda e, kc=kc, h=h, f=f, pbk=pbk: e.matmul(PS[pbk][:, 0:512], lhsT=wpl[:, kc, f * 128:(f + 1) * 128], rhs=pTb[:, kc, h * 512:(h + 1) * 512],
                                                                            start=(kc == 0), stop=(kc == 1)),
                             reads=[bwpl, bpT], writes=[bPS[pbk]])
                    S.op("act", lambda e, q=q, gbk=gbk: e.activation(out=sg[q], in_=PS[gbk][:, 0:512], func=AF.Sigmoid), reads=[bPS[gbk]], writes=[bsg[q]])
                    S.op("dve", lambda e, q=q, f=f, pbk=pbk: e.tensor_tensor(out=tTh[:, f, :], in0=sg[q], in1=PS[pbk][:, 0:512], op=ALU.mult),
                         reads=[bsg[q], bPS[pbk]], writes=[bt[f][h]])
            post_norm_residual(NT2, tTh, bt, 5, None, None, fT, bff, final_out=out_d, halves=(h,), scol=lambda h_: slice(0, 512))
        S.wait_all("sp", toks_out)
        S.emit()
    return nc


def _fm(a):
    T, C = a.shape
    return np.ascontiguousarray(a.reshape(T, C // 128, 128).transpose(2, 1, 0))


def _wblk(w, cols):
    K = w.shape[0]
    return np.ascontiguousarray(w[:, cols].reshape(K // 128, 128, len(cols)).transpose(1, 0, 2))


def prep_inputs(inp):
    f32 = np.float32
    x = np.asarray(inp["x"], f32)[0]
    p = np.asarray(inp["p"], f32)[0, 0]
    g = lambda k: np.asarray(inp[k], f32)[0]
    w_in = g("w_in")
    cols = lambda a, b: np.arange(a, b)
    blks = [cols(0, 512), cols(512, 1024), cols(1024, 1536), cols(1536, 2048), cols(2048, 2560), cols(2592, 3104), cols(3104, 3616)]
    w_in_b = np.stack([_wblk(w_in, c) for c in blks])
    w_dt = _wblk(w_in, cols(2560, 2592))
    w_out = g("w_out")
    w_out_b = np.stack([_wblk(w_out, cols(i * 512, (i + 1) * 512)) for i in range(4)])
    pw = g("pool_w")
    pool_w = np.ascontiguousarray(pw.reshape(4, 2, 128, 256).transpose(2, 0, 1, 3))
    w_up = g("w_ffn_up")
    w_up_b = np.stack([_wblk(w_up, np.concatenate([cols(j * 128, (j + 1) * 128), cols(DFF + j * 128, DFF + (j + 1) * 128)])) for j in range(NFB)])
    w_dn = g("w_ffn_down")
    w_dn_b = np.stack([_wblk(w_dn, cols(m * 128, (m + 1) * 128)) for m in range(NB)])
    w_pg = g("w_ple_gate")
    w_pg_b = np.stack([_wblk(w_pg, cols(i * 256, (i + 1) * 256)) for i in range(8)])
    w_pl = _wblk(g("w_ple"), cols(0, D))
    cst = np.zeros((128, NCST), f32)
    gains = [g("mix_norm_pre"), g("mix_norm_post"), g("ffn_norm_pre"), g("ffn_norm_post"), g("ple_norm_pre"), g("ple_norm_post")]
    for n, gv in enumerate(gains):
        cst[:, C_G + n * 16:C_G + (n + 1) * 16] = gv.reshape(16, 128).T
    cw = g("ssd_conv_w")
    cst[:, C_CW:C_CW + 60] = cw.reshape(5, 12, 128).transpose(2, 1, 0).reshape(128, 60)
    cst[:, C_CB:C_CB + 12] = g("ssd_conv_b").reshape(12, 128).T
    cst[:, C_DTB:C_DTB + 32] = g("ssd_dt_bias").reshape(1, 32)
    cst[:, C_AL:C_AL + 32] = g("ssd_a_log").reshape(1, 32)
    cst[:, C_DS:C_DS + 16] = g("ssd_d").reshape(1, 16)
    cst[:, C_PS:C_PS + 8] = g("pool_scale").reshape(8, 128).T
    fw = g("ffn_conv_w")
    cst[:, C_FW:C_FW + 132] = fw.reshape(3, NFB, 128).transpose(2, 1, 0).reshape(128, 132)
    cst[:, C_FB:C_FB + NFB] = g("ffn_conv_b").reshape(NFB, 128).T
    cmat = np.zeros((128, 6, 128), f32)
    r = np.arange(128)
    cmat[:, 0] = np.eye(128)
    cmat[:, 1] = (r[:, None] <= r[None, :])
    cmat[:, 2] = (r[:, None] >= r[None, :])
    cmat[:, 3] = np.where(r[:, None] <= r[None, :], 0.0, NEG)
    cmat[:, 4] = np.where(r[:, None] >= r[None, :], 0.0, NEG)
    cmat[:, 5] = 1.0
    nrm = np.ascontiguousarray(np.broadcast_to(g("ssd_norm").reshape(1, 1024), (128, 1024)))
    shared = dict(cmat=cmat, nrm=nrm, w_in=w_in_b, w_dt=w_dt, w_out=w_out_b, pool_w=pool_w, w_up=w_up_b, w_dn=w_dn_b, w_pg=w_pg_b, w_pl=w_pl)
    xpad = np.zeros((L + 16, D), f32)
    xpad[8:8 + L] = x
    maps = []
    for k in range(NCORE):
        s0 = k * TOK
        c = cst.copy()
        for gi, kw in enumerate((2, 4, 8, 16)):
            for side in range(2):
                for i in range(8):
                    t = s0 + i if side == 0 else s0 + TOK - 8 + i
                    lo = max(t - kw // 2, 0)
                    hi = min(t + (kw - kw // 2), L)
                    c[:, C_PC + gi * 16 + side * 8 + i] = float(kw) / float(hi - lo)
        for j in range(NCORE):
            c[:, C_V + j * 32:C_V + j * 32 + 16] = 1.0 if j < k else 0.0
            c[:, C_V + j * 32 + 16:C_V + j * 32 + 32] = 1.0 if j > k else 0.0
        if k > 0:
            c[:, C_OH + k - 1] = 1.0
        if k < NCORE - 1:
            c[:, C_OH + 8 + k + 1] = 1.0
        for i in range(NCORE):
            for j in range(NCORE):
                c[i, C_M + j] = 1.0 if (j < i < k) else 0.0
                c[i, C_M + 8 + j] = 1.0 if (k < i < j) else 0.0
        xe = xpad[s0:s0 + TOK + 16]
        m = dict(shared)
        m["xT"] = _fm(xe[8:8 + TOK])
        m["xhT"] = _fm(np.concatenate([xe[0:8], xe[8 + TOK:16 + TOK]], axis=0))
        m["pT"] = _fm(p[s0:s0 + TOK])
        m["cst"] = c
        maps.append(m)
    return maps


_NC_CACHE = {}


def kernel(**inputs):
    maps = prep_inputs(inputs)
    if "nc" not in _NC_CACHE:
        _NC_CACHE["nc"] = build(0)
    nc = _NC_CACHE["nc"]
    res = run_bass_kernel_spmd(nc, maps, core_ids=list(range(NCORE)))
    out = np.empty((1, L, D), np.float32)
    for k in range(NCORE):
        o = np.asarray(res.results[k]["out"], np.float32)
        out[0, k * TOK:(k + 1) * TOK, :] = o.transpose(2, 1, 0).reshape(TOK, D)
    return out
```
